# Optimizing a Trainium2 kernel written in Bass

```python
import math
import jax, jax.numpy as jnp
from jax import lax
import numpy as np

D_MODEL = 2048
BATCH = 4
SEQ = 2048
DEPTH = 2

GRID_W = 64
CTX_LEN = 256
HEAD_DIM = 64
N_BRANCH = 4
BRANCH_W = D_MODEL // N_BRANCH
A_HEADS = BRANCH_W // HEAD_DIM
A_KV_HEADS = A_HEADS // 4
A_WINDOW = 128
A_BLOCK = 128
B_HEADS = BRANCH_W // HEAD_DIM
B_WIN_ROWS_MAX = 8
B_WIN_COLS = 16
B_QCOLS = 16
B_KCOLS = 32
C_WINDOWS = (2, 4, 8, 16)
C_GROUPS = len(C_WINDOWS)
C_GROUP_DIM = BRANCH_W // C_GROUPS
D_HEADS = BRANCH_W // (2 * HEAD_DIM)
D_BLOCK = 128
D_FF = 4 * D_MODEL
ROPE_BASE = 10000.0
EPS = 1e-6
NEG = -1e30

A_W = A_HEADS * HEAD_DIM
A_KV_W = A_KV_HEADS * HEAD_DIM
B_W = B_HEADS * HEAD_DIM
C_W = C_GROUPS * C_GROUP_DIM
D_W = D_HEADS * 2 * HEAD_DIM
SPLIT_SIZES = (A_W, A_KV_W, A_KV_W, B_W, B_W, B_W, C_W, D_W, D_W, D_W, N_BRANCH * D_MODEL)
SPLIT_POINTS = tuple(int(v) for v in np.cumsum(SPLIT_SIZES)[:-1])
PROJ_W = int(sum(SPLIT_SIZES))

kernel_name = "hybrid_gated_parallel_mixers_dit"

F32 = jnp.float32


def rms_norm(x, g):
    xf = x.astype(F32)
    y = xf * lax.rsqrt(jnp.mean(xf * xf, axis=-1, keepdims=True) + EPS)
    return (y * g.astype(F32)).astype(x.dtype)


def modulate(h, shift, scale):
    return h * (1 + scale) + shift


def axial_rope(n):
    t = jnp.arange(n)
    row = (t // GRID_W).astype(F32)
    col = (t % GRID_W).astype(F32)
    n_freq = HEAD_DIM // 4
    inv = ROPE_BASE ** (-jnp.arange(n_freq, dtype=F32) / n_freq)
    ang = jnp.concatenate([row[:, None] * inv, col[:, None] * inv], axis=-1)
    return jnp.cos(ang), jnp.sin(ang)


def rope2d(x, cos, sin):
    q = HEAD_DIM // 4
    xf = x.astype(F32)
    x1 = jnp.concatenate([xf[..., :q], xf[..., 2 * q:3 * q]], axis=-1)
    x2 = jnp.concatenate([xf[..., q:2 * q], xf[..., 3 * q:]], axis=-1)
    c = cos[:, None, :]
    s = sin[:, None, :]
    y1 = x1 * c - x2 * s
    y2 = x2 * c + x1 * s
    return jnp.concatenate([y1[..., :q], y2[..., :q], y1[..., q:], y2[..., q:]], axis=-1).astype(x.dtype)


def split_heads(z, qk_g, rope=None):
    B_, n, _ = z.shape
    aq, ak, av, bq, bk, bv, cu, dq, dk, dv, gz = jnp.split(z, SPLIT_POINTS, axis=-1)
    aq = rms_norm(aq.reshape(B_, n, A_HEADS, HEAD_DIM), qk_g[0, 0])
    ak = rms_norm(ak.reshape(B_, n, A_KV_HEADS, HEAD_DIM), qk_g[0, 1])
    bq = rms_norm(bq.reshape(B_, n, B_HEADS, HEAD_DIM), qk_g[1, 0])
    bk = rms_norm(bk.reshape(B_, n, B_HEADS, HEAD_DIM), qk_g[1, 1])
    dq = rms_norm(dq.reshape(B_, n, 2 * D_HEADS, HEAD_DIM), qk_g[2, 0])
    dk = rms_norm(dk.reshape(B_, n, 2 * D_HEADS, HEAD_DIM), qk_g[2, 1])
    if rope is not None:
        cos, sin = rope
        aq = rope2d(aq, cos, sin)
        ak = rope2d(ak, cos, sin)
        dq = rope2d(dq, cos, sin)
        dk = rope2d(dk, cos, sin)
    av = av.reshape(B_, n, A_KV_HEADS, HEAD_DIM)
    bv = bv.reshape(B_, n, B_HEADS, HEAD_DIM)
    dq = dq.reshape(B_, n, D_HEADS, 2, HEAD_DIM)
    dk = dk.reshape(B_, n, D_HEADS, 2, HEAD_DIM)
    dv = dv.reshape(B_, n, D_HEADS, 2 * HEAD_DIM)
    return (aq, ak, av, bq, bk, bv, cu, dq, dk, dv, gz)


def context_attn(q, k, v, sink=None):
    B_, nq, H, Dh = q.shape
    G = k.shape[2]
    R = H // G
    nk = k.shape[1]
    qg = q.reshape(B_, nq, G, R, Dh)
    s = jnp.einsum('bqgrd,bkgd->bgrqk', qg, k, preferred_element_type=F32) * Dh ** -0.5
    if sink is not None:
        s_sink = jnp.broadcast_to(sink.astype(F32).reshape(G, R)[None, :, :, None, None], s.shape[:-1] + (1,))
        s = jnp.concatenate([s, s_sink], axis=-1)
    p = jax.nn.softmax(s, axis=-1)[..., :nk]
    o = jnp.einsum('bgrqk,bkgd->bqgrd', p.astype(v.dtype), v)
    return o.reshape(B_, nq, H * Dh)


def window_gqa(q, k, v, kc, vc, sink):
    B_, S, _, _ = q.shape
    nb = S // A_BLOCK
    G, R = A_KV_HEADS, A_HEADS // A_KV_HEADS
    nc = kc.shape[1]
    qb = q.reshape(B_, nb, A_BLOCK, G, R, HEAD_DIM)

    def band(t):
        tb = t.reshape(B_, nb, A_BLOCK, G, HEAD_DIM)
        tp = jnp.pad(tb, ((0, 0), (1, 1), (0, 0), (0, 0), (0, 0)))
        return jnp.concatenate([tp[:, :-2], tp[:, 1:-1], tp[:, 2:]], axis=2)

    kb, vb = band(k), band(v)
    scale = HEAD_DIM ** -0.5
    s_lat = jnp.einsum('bnqgrd,bnkgd->bngrqk', qb, kb, preferred_element_type=F32) * scale
    qpos = jnp.arange(S).reshape(nb, A_BLOCK)
    kpos = (jnp.arange(nb)[:, None] - 1) * A_BLOCK + jnp.arange(3 * A_BLOCK)[None, :]
    valid = ((jnp.abs(qpos[:, :, None] - kpos[:, None, :]) <= A_WINDOW)
             & (kpos[:, None, :] >= 0) & (kpos[:, None, :] < S))
    s_lat = jnp.where(valid[None, :, None, None], s_lat, NEG)
    s_ctx = jnp.einsum('bnqgrd,bkgd->bngrqk', qb, kc, preferred_element_type=F32) * scale
    s_sink = jnp.broadcast_to(sink.astype(F32).reshape(G, R)[None, None, :, :, None, None], s_lat.shape[:-1] + (1,))
    p = jax.nn.softmax(jnp.concatenate([s_lat, s_ctx, s_sink], axis=-1), axis=-1).astype(v.dtype)
    L = 3 * A_BLOCK
    o = (jnp.einsum('bngrqk,bnkgd->bnqgrd', p[..., :L], vb)
         + jnp.einsum('bngrqk,bkgd->bnqgrd', p[..., L:L + nc], vc))
    return o.reshape(B_, S, A_W)


def neighbourhood_attn(q, k, v, kc, vc, rpb):
    B_, S, H, Dh = q.shape
    rows = S // GRID_W
    kh = min(B_WIN_ROWS_MAX, rows)
    ncb = GRID_W // B_QCOLS
    nc = kc.shape[1]
    r = jnp.arange(rows)
    rs = jnp.clip(r - kh // 2, 0, rows - kh)
    key_rows = rs[:, None] + jnp.arange(kh)[None, :]
    cs = jnp.clip(jnp.arange(GRID_W) - B_WIN_COLS // 2, 0, GRID_W - B_WIN_COLS)
    kb0 = jnp.clip(jnp.arange(ncb) * B_QCOLS - B_WIN_COLS // 2, 0, GRID_W - B_KCOLS)
    key_cols = kb0[:, None] + jnp.arange(B_KCOLS)[None, :]
    qcol = jnp.arange(ncb)[:, None] * B_QCOLS + jnp.arange(B_QCOLS)[None, :]

    def gather(t):
        tg = t.reshape(B_, rows, GRID_W, H, Dh)
        return tg[:, key_rows[:, :, None, None], key_cols[None, None, :, :]]

    kg, vg = gather(k), gather(v)
    qg = q.reshape(B_, rows, ncb, B_QCOLS, H, Dh)
    scale = Dh ** -0.5
    s = jnp.einsum('brjqhd,brkjchd->bhrjqkc', qg, kg, preferred_element_type=F32) * scale
    cs_q = cs[qcol]
    kcol = key_cols[:, None, :]
    valid = (kcol >= cs_q[:, :, None]) & (kcol < cs_q[:, :, None] + B_WIN_COLS)
    ri = key_rows - r[:, None] + (B_WIN_ROWS_MAX - 1)
    ci = jnp.clip(kcol - qcol[:, :, None] + (B_WIN_COLS - 1), 0, 2 * B_WIN_COLS - 2)
    bias = rpb[:, ri[:, None, None, :, None], ci[None, :, :, None, :]]
    s = s + bias.astype(F32)[None]
    s = jnp.where(valid[None, None, None, :, :, None, :], s, NEG)
    s = s.reshape(B_, H, rows, ncb, B_QCOLS, kh * B_KCOLS)
    s_ctx = jnp.einsum('brjqhd,bkhd->bhrjqk', qg, kc, preferred_element_type=F32) * scale
    p = jax.nn.softmax(jnp.concatenate([s, s_ctx], axis=-1), axis=-1).astype(v.dtype)
    nl = kh * B_KCOLS
    p_lat = p[..., :nl].reshape(B_, H, rows, ncb, B_QCOLS, kh, B_KCOLS)
    o = (jnp.einsum('bhrjqkc,brkjchd->brjqhd', p_lat, vg)
         + jnp.einsum('bhrjqk,bkhd->brjqhd', p[..., nl:nl + nc], vc))
    return o.reshape(B_, S, B_W)


def multiscale_pool(u):
    n = u.shape[1]
    uf = u.astype(F32)
    csum = jnp.pad(jnp.cumsum(uf, axis=1), ((0, 0), (1, 0), (0, 0)))
    t = jnp.arange(n)
    outs = []
    for g, w in enumerate(C_WINDOWS):
        lo = jnp.clip(t - w // 2, 0, n - 1)
        hi = jnp.clip(t + w - 1 - w // 2, 0, n - 1)
        cg = csum[..., g * C_GROUP_DIM:(g + 1) * C_GROUP_DIM]
        cnt = (hi - lo + 1).astype(F32)[None, :, None]
        outs.append((cg[:, hi + 1] - cg[:, lo]) / cnt)
    return (jnp.concatenate(outs, axis=-1) - uf).astype(u.dtype)


def pool_branch(u, w, scale):
    B_, n, _ = u.shape
    pooled = multiscale_pool(u).reshape(B_, n, C_GROUPS, C_GROUP_DIM)
    y = jnp.einsum('bngc,gce->bnge', pooled, w).reshape(B_, n, C_W)
    return y * scale


def diff_core(q, k, v, lam):
    s = jnp.einsum('bqhmd,bkhmd->bhmqk', q, k, preferred_element_type=F32) * HEAD_DIM ** -0.5
    p = jax.nn.softmax(s, axis=-1)
    a = p[:, :, 0] - lam * p[:, :, 1]
    return jnp.einsum('bhqk,bkhe->bqhe', a.astype(v.dtype), v)


def diff_post(o, g, lam_init):
    B_, n = o.shape[:2]
    return (rms_norm(o, g) * (1 - lam_init)).reshape(B_, n, D_W)


def diff_latent(q, k, v, kc, vc, lam):
    B_, S = q.shape[:2]
    nb = S // D_BLOCK
    kall = jnp.concatenate([k, kc], axis=1)
    vall = jnp.concatenate([v, vc], axis=1)
    qb = q.reshape(B_, nb, D_BLOCK, D_HEADS, 2, HEAD_DIM).swapaxes(0, 1)
    o = lax.map(lambda qblk: diff_core(qblk, kall, vall, lam), qb)
    return o.swapaxes(0, 1).reshape(B_, S, D_HEADS, 2 * HEAD_DIM)


def merge_branches(ys, gz, b_gate, w_branch, w_out):
    B_, n, _ = gz.shape
    g = jax.nn.sigmoid((gz + b_gate).astype(F32)).astype(gz.dtype).reshape(B_, n, N_BRANCH, D_MODEL)
    proj = jnp.einsum('bnkw,kwd->bnkd', jnp.stack(ys, axis=2), w_branch)
    return jnp.sum(g * proj, axis=2) @ w_out


def sq_relu_mlp(h, w1, w2):
    return jnp.square(jax.nn.relu(h @ w1)) @ w2


def setup_inputs(seed: int = 0) -> dict:
    key = jax.random.key(seed)
    ks = jax.random.split(key, 24)

    def nrm(k, shape, s):
        return s * jax.random.normal(k, shape, F32)

    return {
        "x": nrm(ks[0], (BATCH, SEQ, D_MODEL), 1.0),
        "c": nrm(ks[1], (BATCH, D_MODEL), 1.0),
        "ctx": nrm(ks[2], (BATCH, CTX_LEN, D_MODEL), 1.0),
        "c_ctx": nrm(ks[3], (D_MODEL,), 1.0),
        "w_ada": nrm(ks[4], (DEPTH, D_MODEL, 6 * D_MODEL), 0.5 * D_MODEL ** -0.5),
        "b_ada": nrm(ks[5], (DEPTH, 6 * D_MODEL), 0.01),
        "g_norm1": 1.0 + nrm(ks[6], (DEPTH, D_MODEL), 0.05),
        "g_norm2": 1.0 + nrm(ks[7], (DEPTH, D_MODEL), 0.05),
        "w_in": nrm(ks[8], (DEPTH, D_MODEL, PROJ_W), D_MODEL ** -0.5),
        "b_gate": nrm(ks[9], (DEPTH, N_BRANCH * D_MODEL), 0.01),
        "qk_gain": 1.0 + nrm(ks[10], (DEPTH, 3, 2, HEAD_DIM), 0.05),
        "a_sink": nrm(ks[11], (DEPTH, A_HEADS), 0.5),
        "b_rpb": nrm(ks[12], (DEPTH, B_HEADS, 2 * B_WIN_ROWS_MAX - 1, 2 * B_WIN_COLS - 1), 0.1),
        "c_w": nrm(ks[13], (DEPTH, C_GROUPS, C_GROUP_DIM, C_GROUP_DIM), C_GROUP_DIM ** -0.5),
        "c_scale": 1.0 + nrm(ks[14], (DEPTH, C_W), 0.1),
        "d_lambda": nrm(ks[15], (DEPTH, 4, HEAD_DIM), 0.1),
        "d_subln": 1.0 + nrm(ks[16], (DEPTH, 2 * HEAD_DIM), 0.05),
        "w_branch": nrm(ks[17], (DEPTH, N_BRANCH, BRANCH_W, D_MODEL), BRANCH_W ** -0.5),
        "w_out": nrm(ks[18], (DEPTH, D_MODEL, D_MODEL), D_MODEL ** -0.5),
        "w_ff1": nrm(ks[19], (DEPTH, D_MODEL, D_FF), D_MODEL ** -0.5),
        "w_ff2": nrm(ks[20], (DEPTH, D_FF, D_MODEL), D_FF ** -0.5),
    }


def reference(x, c, ctx, c_ctx, w_ada, b_ada, g_norm1, g_norm2, w_in, b_gate, qk_gain, a_sink, b_rpb,
              c_w, c_scale, d_lambda, d_subln, w_branch, w_out, w_ff1, w_ff2):
    n = x.shape[1]
    rope = axial_rope(n)
    xc = ctx
    for l in range(DEPTH):
        last = l == DEPTH - 1
        mod = (jax.nn.silu(c) @ w_ada[l] + b_ada[l])[:, None, :]
        mod_c = (jax.nn.silu(c_ctx) @ w_ada[l] + b_ada[l])[None, None, :]
        sh1, sc1, gt1, sh2, sc2, gt2 = jnp.split(mod, 6, axis=-1)
        sh1c, sc1c, gt1c, sh2c, sc2c, gt2c = jnp.split(mod_c, 6, axis=-1)
        lam_init = 0.8 - 0.6 * math.exp(-0.3 * l)
        dl = d_lambda[l].astype(F32)
        lam = jnp.exp(jnp.sum(dl[0] * dl[1])) - jnp.exp(jnp.sum(dl[2] * dl[3])) + lam_init

        h = modulate(rms_norm(x, g_norm1[l]), sh1, sc1)
        hc = modulate(rms_norm(xc, g_norm1[l]), sh1c, sc1c)
        aq, ak, av, bq, bk, bv, cu, dq, dk, dv, gz = split_heads(h @ w_in[l], qk_gain[l], rope)
        aqc, akc, avc, bqc, bkc, bvc, cuc, dqc, dkc, dvc, gzc = split_heads(hc @ w_in[l], qk_gain[l])
        ya = window_gqa(aq, ak, av, akc, avc, a_sink[l])
        yb = neighbourhood_attn(bq, bk, bv, bkc, bvc, b_rpb[l])
        yc = pool_branch(cu, c_w[l], c_scale[l])
        yd = diff_post(diff_latent(dq, dk, dv, dkc, dvc, lam), d_subln[l], lam_init)
        x = x + gt1 * merge_branches([ya, yb, yc, yd], gz, b_gate[l], w_branch[l], w_out[l])
        if not last:
            yac = context_attn(aqc, akc, avc, a_sink[l])
            ybc = context_attn(bqc, bkc, bvc)
            ycc = pool_branch(cuc, c_w[l], c_scale[l])
            ydc = diff_post(diff_core(dqc, dkc, dvc, lam), d_subln[l], lam_init)
            xc = xc + gt1c * merge_branches([yac, ybc, ycc, ydc], gzc, b_gate[l], w_branch[l], w_out[l])

        h2 = modulate(rms_norm(x, g_norm2[l]), sh2, sc2)
        x = x + gt2 * sq_relu_mlp(h2, w_ff1[l], w_ff2[l])
        if not last:
            h2c = modulate(rms_norm(xc, g_norm2[l]), sh2c, sc2c)
            xc = xc + gt2c * sq_relu_mlp(h2c, w_ff1[l], w_ff2[l])
    return x
```

```python
import contextlib
import numpy as np
import ml_dtypes
import concourse.bass as bass
import concourse.mybir as mybir
from concourse.bass_utils import run_bass_kernel_spmd

F32 = mybir.dt.float32
BF16 = mybir.dt.bfloat16
ACTF = mybir.ActivationFunctionType
ALU = mybir.AluOpType

D = 2048
NCH = 16
SEQ = 2048
CTX = 256
TL = 1024
TC = 128
T = TL + TC
EPS = 1e-6
PROJ_W = 12544
GATE0 = 4352
O_AQ, O_AK, O_AV, O_BQ, O_BK, O_BV, O_CU, O_DQ, O_DK, O_DV = 0, 512, 640, 768, 1280, 1792, 2304, 2816, 3328, 3840
TB = [(0, 512), (512, 512), (1024, 128)]


class Key:
    __slots__ = ("name", "w", "r")

    def __init__(self, name):
        self.name = name
        self.w = None
        self.r = {}


class T_:
    def __init__(self, t, name):
        self.t = t
        self.k = Key(name)

    def __getitem__(self, idx):
        return self.t[idx]


class Ring:
    def __init__(self, items):
        self.items = items
        self.i = 0

    def next(self):
        it = self.items[self.i]
        self.i = (self.i + 1) % len(self.items)
        return it


class Sched:
    ENG = ("pe", "act", "dve", "pool", "sp")

    def __init__(self, nc, stack, n_dma_sems=40):
        self.nc = nc
        self.stack = stack
        self.q = {e: [] for e in self.ENG}
        self.sem = {e: stack.enter_context(nc.semaphore("c_" + e)) for e in self.ENG}
        self.cnt = {e: 0 for e in self.ENG}
        self.pending = {e: False for e in self.ENG}
        self.seen = {e: {} for e in self.ENG}
        self.dsem = [stack.enter_context(nc.semaphore("d%d" % i)) for i in range(n_dma_sems)]
        self.dval = [0] * n_dma_sems
        self.dnext = 0
        self.dnext_sw = 0
        self.n_sw_sems = 8
        self.uid = 0
        self.lo = self.SB_LO
        self.hi = self.SB_HI
        self.peak = 0

    SB_LO = 16384 + 1024
    SB_HI = 229376 - 64

    def sb(self, name, shape, dt, top=False):
        self.uid += 1
        name = "%s_%d" % (name, self.uid)
        n = 1
        for d in shape[1:]:
            n *= d
        nbytes = (n * mybir.dt.size(dt) + 63) // 64 * 64
        if top:
            self.hi -= nbytes
            off = self.hi
        else:
            off = self.lo
            self.lo += nbytes
        assert self.lo <= self.hi, "SBUF arena overflow at %s: lo=%d hi=%d" % (name, self.lo, self.hi)
        self.peak = max(self.peak, self.lo + (self.SB_HI - self.hi))
        return T_(self.nc.alloc_sbuf_tensor_at(name, list(shape), dt, offset=off), name)

    def mark(self):
        return (self.lo, self.hi)

    def release(self, m, top_only=False, bottom_only=False):
        if not top_only:
            self.lo = m[0]
        if not bottom_only:
            self.hi = m[1]

    def ps(self, name, shape, dt=F32):
        return T_(self.stack.enter_context(self.nc.psum_tensor(name, list(shape), dt)), name)

    def ring(self, name, shape, dt, n):
        return Ring([self.sb("%s%d" % (name, i), shape, dt) for i in range(n)])

    @staticmethod
    def _k(t):
        return t.k if isinstance(t, T_) else t

    def _need(self, e, tickets):
        for (sk, val) in tickets:
            if self.seen[e].get(sk, 0) >= val:
                continue
            self.seen[e][sk] = val
            self.q[e].append(("wait", sk, val))

    def _deps(self, e, reads, writes):
        deps = []
        for t in reads:
            k = self._k(t)
            if k.w is not None:
                deps.append(k.w)
        for t in writes:
            k = self._k(t)
            if k.w is not None:
                deps.append(k.w)
            deps.extend(k.r.values())
        if e == "pe":
            deps = [d for d in deps if d[0] != ("e", "pe")]
        return deps

    def _mark(self, ticket, reads, writes):
        for t in writes:
            k = self._k(t)
            k.w = ticket
            k.r = {}
        for t in reads:
            k = self._k(t)
            k.r[ticket[0]] = ticket

    def op(self, e, fn, reads=(), writes=(), inc=True):
        self._need(e, self._deps(e, reads, writes))
        if inc:
            self.cnt[e] += 1
            ticket = (("e", e), self.cnt[e])
            self.pending[e] = False
        else:
            ticket = (("e", e), self.cnt[e] + 1)
            self.pending[e] = True
        self.q[e].append(("op", fn, inc))
        self._mark(ticket, reads, writes)
        return ticket

    def dma(self, qe, out, in_, reads=(), writes=()):
        nsw = self.n_sw_sems
        if qe == "pool":
            i = self.dnext_sw
            self.dnext_sw = (self.dnext_sw + 1) % nsw
        else:
            i = nsw + self.dnext
            self.dnext = (self.dnext + 1) % (len(self.dsem) - nsw)
        deps = self._deps(qe, reads, writes)
        if self.dval[i] > 0:
            deps.append((("d", i), self.dval[i]))
        self._need(qe, deps)
        self.dval[i] += 16
        ticket = (("d", i), self.dval[i])
        self.q[qe].append(("dma", out, in_, i))
        self._mark(ticket, reads, writes)
        return ticket

    def wait_all(self, e, keys):
        deps = []
        for t in keys:
            k = self._k(t)
            if k.w is not None:
                deps.append(k.w)
        self._need(e, deps)

    def _semof(self, sk):
        return self.sem[sk[1]] if sk[0] == "e" else self.dsem[sk[1]]

    def barrier(self):
        for e in self.ENG:
            deps = [(("e", e2), self.cnt[e2]) for e2 in self.ENG if e2 != e and self.cnt[e2] > 0]
            deps += [(("d", i), v) for i, v in enumerate(self.dval) if v > 0]
            self._need(e, deps)

    def _check_deadlock(self):
        if not hasattr(self, "_simval"):
            self._simval = {}
        val = self._simval
        pos = {e: 0 for e in self.ENG}
        progress = True
        while progress:
            progress = False
            for e in self.ENG:
                q = self.q[e]
                while pos[e] < len(q):
                    it = q[pos[e]]
                    if it[0] == "wait":
                        if val.get(it[1], 0) < it[2]:
                            break
                    elif it[0] == "op":
                        if it[2]:
                            val[("e", e)] = val.get(("e", e), 0) + 1
                    else:
                        val[("d", it[3])] = val.get(("d", it[3]), 0) + 16
                    pos[e] += 1
                    progress = True
        stuck = {e: (pos[e], len(self.q[e]), self.q[e][pos[e]][:3]) for e in self.ENG if pos[e] < len(self.q[e])}
        assert not stuck, "semaphore deadlock: %s ; sems=%s" % (stuck, {k: v for k, v in val.items() if k[0] == "e"})

    def flush(self):
        for e in self.ENG:
            assert not self.pending[e], "engine %s ends with a non-incrementing op" % e
        self._check_deadlock()
        with self.nc.Block() as block:
            def run(e):
                def body(engine):
                    for item in self.q[e]:
                        if item[0] == "wait":
                            engine.wait_ge(self._semof(item[1]), item[2])
                        elif item[0] == "op":
                            ins = item[1](engine)
                            if item[2]:
                                ins.then_inc(self.sem[e], 1)
                        else:
                            _, out, in_, i = item
                            engine.dma_start(out=out, in_=in_).then_inc(self.dsem[i], 16)
                return body
            block.tensor(run("pe"))
            block.scalar(run("act"))
            block.vector(run("dve"))
            block.gpsimd(run("pool"))
            block.sync(run("sp"))
        self.q = {e: [] for e in self.ENG}


class Ctx:
    def __init__(self, nc, stack):
        self.nc = nc
        self.S = Sched(nc, stack)
        S = self.S
        self.psr = Ring([S.ps("ps%d" % i, [128, 512]) for i in range(8)])
        self.ident = S.sb("ident", [128, 128], F32)
        self.ones = S.sb("ones", [128, 128], BF16)
        self.eps = S.sb("eps", [128, 1], F32)
        S.op("dve", lambda e: e.memset(self.ones[:, :], 1.0), writes=[self.ones])
        S.op("dve", lambda e: e.memset(self.eps[:, :], EPS), writes=[self.eps])
        self.outkeys = []

    def din(self, name, shape, dt=F32):
        return self.nc.dram_tensor(name, list(shape), dt, kind="ExternalInput").ap()

    def dout(self, name, shape, dt=F32):
        ap = self.nc.dram_tensor(name, list(shape), dt, kind="ExternalOutput").ap()
        k = Key(name)
        self.outkeys.append(k)
        return ap, k

    def finish(self):
        self.S.wait_all("sp", self.outkeys)
        self.S.flush()


def mm(S, ps_ap, lhsT, rhs, start, stop, reads, ps, inc=None):
    S.op("pe", lambda e: e.matmul(ps_ap, lhsT=lhsT, rhs=rhs, start=start, stop=stop),
         reads=reads, writes=[ps], inc=(stop if inc is None else inc))


def load_vecT(C, rows_aps, name):
    S = C.S
    n = sum(a.shape[0] for a in rows_aps)
    assert n <= 128
    st = S.sb(name + "_st", [n, 128], F32)
    o = 0
    for a in rows_aps:
        S.dma("sp", st[o:o + a.shape[0], :], a, writes=[st])
        o += a.shape[0]
    ps = C.psr.next()
    S.op("pe", lambda e: e.matmul(ps[:, 0:n], lhsT=st[:, :], rhs=C.ident[0:n, 0:n], start=True, stop=True),
         reads=[st, C.ident], writes=[ps])
    out = S.sb(name, [128, n], F32)
    S.op("dve", lambda e: e.tensor_copy(out=out[:, :], in_=ps[:, 0:n]), reads=[ps], writes=[out])
    return out


def emit_silu_c(C, cvec):
    S = C.S
    cT = load_vecT(C, [cvec[0:1, :].rearrange("o (k p) -> (o k) p", p=128),
                       cvec[1:2, :].rearrange("o (k p) -> (o k) p", p=128)], "cT")
    siluT = S.sb("siluT", [128, NCH, 2], BF16)
    for r in range(2):
        S.op("act", lambda e, r=r: e.activation(out=siluT[:, :, r], in_=cT[:, r * 16:(r + 1) * 16], func=ACTF.Silu),
             reads=[cT], writes=[siluT])
    return siluT


def emit_ada(C, siluT, wada, bada_T, boff, col0, nsec, wring, name, modT=None):
    S = C.S
    if modT is None:
        modT = S.sb(name, [128, nsec * NCH, 2], F32)
    for blk in range(nsec * 4):
        wt = wring.next()
        c0 = col0 * D + blk * 512
        S.dma("pool", wt[:, :, :], wada[:, c0:c0 + 512].rearrange("(k p) n -> p k n", p=128), writes=[wt])
        for j in range(4):
            ps = C.psr.next()
            for kc in range(NCH):
                mm(S, ps[:, 0:2], wt[:, kc, j * 128:(j + 1) * 128], siluT[:, kc, :], kc == 0, kc == NCH - 1,
                   [wt, siluT], ps)
            jj = blk * 4 + j
            S.op("dve", lambda e, ps=ps, jj=jj: e.tensor_scalar(out=modT[:, jj, :], in0=ps[:, 0:2],
                                                                scalar1=bada_T[:, boff + jj:boff + jj + 1], scalar2=None,
                                                                op0=ALU.add),
                 reads=[ps, bada_T], writes=[modT])
    return modT


def emit_norm(C, xT, hT, A, Bv, tmp_ring, sq_ring):
    S = C.S
    pss = [C.psr.next() for _ in TB]
    for dc in range(NCH):
        sq = sq_ring.next()
        S.op("act", lambda e, dc=dc, sq=sq: e.activation(out=sq[:, :], in_=xT[dc][:, :], func=ACTF.Square),
             reads=[xT[dc]], writes=[sq])
        for ti, (t0, tl) in enumerate(TB):
            mm(S, pss[ti][:, 0:tl], C.ones[:, :], sq[:, t0:t0 + tl], dc == 0, dc == NCH - 1, [C.ones, sq], pss[ti],
               inc=True)
    rs = S.sb("rs", [128, T], F32)
    for ti, (t0, tl) in enumerate(TB):
        S.op("act", lambda e, ti=ti, t0=t0, tl=tl: e.activation(out=rs[:, t0:t0 + tl], in_=pss[ti][:, 0:tl],
                                                                func=ACTF.Sqrt, scale=1.0 / D, bias=C.eps[:, 0:1]),
             reads=[pss[ti], C.eps], writes=[rs])
    S.op("dve", lambda e: e.reciprocal(out=rs[:, :], in_=rs[:, :]), reads=[rs], writes=[rs])
    for dc in range(NCH):
        tmp = tmp_ring.next()
        S.op("dve", lambda e, dc=dc, tmp=tmp: e.tensor_tensor(out=tmp[:, :], in0=xT[dc][:, :], in1=rs[:, :], op=ALU.mult),
             reads=[xT[dc], rs], writes=[tmp])
        for r, (t0, tl) in enumerate([(0, TL), (TL, TC)]):
            S.op("act", lambda e, dc=dc, tmp=tmp, r=r, t0=t0, tl=tl: e.activation(
                out=hT[dc][:, t0:t0 + tl], in_=tmp[:, t0:t0 + tl], func=ACTF.Identity,
                scale=A[:, dc, r:r + 1], bias=Bv[:, dc, r:r + 1]),
                reads=[tmp, A, Bv], writes=[hT[dc]])


def emit_AB(C, gT, modT, i_sh, i_sc, name):
    S = C.S
    A = S.sb(name + "A", [128, NCH, 2], F32)
    Bv = S.sb(name + "B", [128, NCH, 2], F32)
    for r in range(2):
        S.op("dve", lambda e, r=r: e.scalar_tensor_tensor(out=A[:, :, r], in0=modT[:, i_sc * 16:(i_sc + 1) * 16, r],
                                                          scalar=1.0, in1=gT[:, 0:16], op0=ALU.add, op1=ALU.mult),
             reads=[modT, gT], writes=[A])
        S.op("dve", lambda e, r=r: e.tensor_copy(out=Bv[:, :, r], in_=modT[:, i_sh * 16:(i_sh + 1) * 16, r]),
             reads=[modT], writes=[Bv])
    return A, Bv


QK_CHUNKS = ([("aq", O_AQ + 128 * i, 0, True) for i in range(4)] + [("ak", O_AK, 1, True)]
             + [("dq", O_DQ + 128 * i, 4, True) for i in range(4)] + [("dk", O_DK + 128 * i, 5, True) for i in range(4)]
             + [("bq", O_BQ + 128 * i, 2, False) for i in range(4)] + [("bk", O_BK + 128 * i, 3, False) for i in range(4)])
NQK = len(QK_CHUNKS)
N_ROPED = 13
TOK_COLS = np.concatenate([np.arange(O_AV, O_AV + 128), np.arange(O_BV, O_BV + 512), np.arange(O_CU, O_CU + 512),
                           np.arange(O_DV, O_DV + 512)])
NTOK = len(TOK_COLS)


def _swap_idx(n):
    i = np.arange(n)
    q = (i % 64) // 16
    return i - (i % 64) + ((q ^ 1) * 16) + (i % 16)


def proj_weight_cols():
    cols = []
    for (_, c0, _, roped) in QK_CHUNKS[:N_ROPED]:
        base = np.arange(c0, c0 + 128)
        cols.append(base)
        cols.append(base[_swap_idx(128)])
    for (_, c0, _, roped) in QK_CHUNKS[N_ROPED:]:
        cols.append(np.arange(c0, c0 + 128))
    cols.append(TOK_COLS)
    return np.concatenate(cols)


NPW = 13 * 256 + 8 * 128 + NTOK


def build_proj(stop=99):
    nc = bass.Bass("TRN2", target_bir_lowering=False)
    with contextlib.ExitStack() as st:
        C = Ctx(nc, st)
        S = C.S
        xT_d = C.din("xT", [D, T])
        cvec = C.din("cvec", [2, D])
        wada = C.din("wada", [D, 6 * D])
        vecs = C.din("vecs", [48 + 12, 128])
        ident_d = C.din("ident", [128, 128])
        ropeC = C.din("ropeC", [128, T])
        ropeS = C.din("ropeS", [128, T])
        bones_d = C.din("bones", [128, 128])
        wp = C.din("wp", [D, NPW])
        qkT_d, k_qkT = C.dout("qkT", [NQK, 128, T], BF16)
        vtok_d, k_vtok = C.dout("vtok", [T, NTOK], BF16)
        hT_o, k_hT = C.dout("hT", [NCH, 128, T], BF16)

        S.dma("sp", C.ident[:, :], ident_d, writes=[C.ident])
        bones_f = S.sb("bones_f", [128, 128], F32)
        bones = S.sb("bones", [128, 128], BF16)
        S.dma("sp", bones_f[:, :], bones_d, writes=[bones_f])
        S.op("dve", lambda e: e.tensor_copy(out=bones[:, :], in_=bones_f[:, :]), reads=[bones_f], writes=[bones])
        rC = S.sb("rC", [128, T], F32)
        rS = S.sb("rS", [128, T], F32)
        S.dma("sp", rC[:, :], ropeC, writes=[rC])
        S.dma("sp", rS[:, :], ropeS, writes=[rS])
        xT = [S.sb("xT%d" % i, [128, T], F32) for i in range(NCH)]
        for dc in range(NCH):
            S.dma("sp", xT[dc][:, :], xT_d[dc * 128:(dc + 1) * 128, :], writes=[xT[dc]])
        hT = [S.sb("hT%d" % i, [128, T], BF16) for i in range(NCH)]
        wring = S.ring("wt", [128, NCH, 512], BF16, 2)
        tmp_ring = S.ring("ntmp", [128, T], F32, 2)
        sq_ring = S.ring("nsq", [128, T], BF16, 2)

        if stop == 0:
            C.finish(); return nc
        vT = load_vecT(C, [vecs], "vT")
        if stop == 1:
            C.finish(); return nc
        siluT = emit_silu_c(C, cvec)
        if stop == 2:
            C.finish(); return nc
        modT = emit_ada(C, siluT, wada, vT, 16, 0, 2, wring, "modT1")
        if stop == 3:
            C.finish(); return nc
        A, Bv = emit_AB(C, vT, modT, 0, 1, "n1")
        emit_norm(C, xT, hT, A, Bv, tmp_ring, sq_ring)
        for dc in range(NCH):
            S.dma("sp", hT_o[dc, :, :], hT[dc][:, :], reads=[hT[dc]], writes=[k_hT])
        if stop == 4:
            C.finish(); return nc

        sqr = S.ring("sq", [128, 512], BF16, 2)
        rsr = S.ring("rsq", [128, 512], F32, 2)
        t1r = S.ring("t1", [128, 512], F32, 2)
        t2r = S.ring("t2", [128, 512], F32, 2)
        outr = S.ring("qko", [128, T], BF16, 2)

        def load_w(c0, w):
            wt = wring.next()
            S.dma("pool", wt[:, :, 0:w], wp[:, c0:c0 + w].rearrange("(k p) n -> p k n", p=128), writes=[wt])
            return wt

        def gemm_cm(wt, o, t0, tl):
            ps = C.psr.next()
            for kc in range(NCH):
                mm(S, ps[:, 0:tl], wt[:, kc, o:o + 128], hT[kc][:, t0:t0 + tl], kc == 0, kc == NCH - 1, [wt, hT[kc]], ps)
            return ps

        def rstd_of(ps, tl):
            sq = sqr.next()
            S.op("act", lambda e: e.activation(out=sq[:, 0:tl], in_=ps[:, 0:tl], func=ACTF.Square), reads=[ps], writes=[sq])
            p2 = C.psr.next()
            mm(S, p2[:, 0:tl], bones[:, :], sq[:, 0:tl], True, True, [bones, sq], p2)
            rs = rsr.next()
            S.op("act", lambda e: e.activation(out=rs[:, 0:tl], in_=p2[:, 0:tl], func=ACTF.Sqrt, scale=1.0 / 64,
                                               bias=C.eps[:, 0:1]), reads=[p2, C.eps], writes=[rs])
            S.op("dve", lambda e: e.reciprocal(out=rs[:, 0:tl], in_=rs[:, 0:tl]), reads=[rs], writes=[rs])
            return rs

        col = 0
        ci = 0
        while ci < N_ROPED:
            npair = min(2, N_ROPED - ci)
            wt = load_w(col, npair * 256)
            for pi in range(npair):
                (nm, c0, gi, roped) = QK_CHUNKS[ci + pi]
                ob = outr.next()
                for (t0, tl) in TB:
                    pz = gemm_cm(wt, pi * 256, t0, tl)
                    pzs = gemm_cm(wt, pi * 256 + 128, t0, tl)
                    rs = rstd_of(pz, tl)
                    t1 = t1r.next()
                    t2 = t2r.next()
                    S.op("dve", lambda e, pz=pz, t1=t1, gi=gi, t0=t0, tl=tl: e.scalar_tensor_tensor(
                        out=t1[:, 0:tl], in0=pz[:, 0:tl], scalar=vT[:, 48 + gi:49 + gi], in1=rC[:, t0:t0 + tl],
                        op0=ALU.mult, op1=ALU.mult), reads=[pz, vT, rC], writes=[t1])
                    S.op("dve", lambda e, pzs=pzs, t2=t2, gi=gi, t0=t0, tl=tl: e.scalar_tensor_tensor(
                        out=t2[:, 0:tl], in0=pzs[:, 0:tl], scalar=vT[:, 54 + gi:55 + gi], in1=rS[:, t0:t0 + tl],
                        op0=ALU.mult, op1=ALU.mult), reads=[pzs, vT, rS], writes=[t2])
                    S.op("pool", lambda e, t1=t1, t2=t2, tl=tl: e.tensor_tensor(out=t1[:, 0:tl], in0=t1[:, 0:tl],
                                                                                in1=t2[:, 0:tl], op=ALU.add),
                         reads=[t1, t2], writes=[t1])
                    S.op("pool", lambda e, t1=t1, rs=rs, ob=ob, t0=t0, tl=tl: e.tensor_tensor(
                        out=ob[:, t0:t0 + tl], in0=t1[:, 0:tl], in1=rs[:, 0:tl], op=ALU.mult),
                        reads=[t1, rs], writes=[ob])
                S.dma("sp", qkT_d[ci + pi, :, :], ob[:, :], reads=[ob], writes=[k_qkT])
            col += npair * 256
            ci += npair
        while ci < NQK:
            wt = load_w(col, 512)
            for pi in range(4):
                (nm, c0, gi, roped) = QK_CHUNKS[ci + pi]
                ob = outr.next()
                for (t0, tl) in TB:
                    pz = gemm_cm(wt, pi * 128, t0, tl)
                    rs = rstd_of(pz, tl)
                    S.op("dve", lambda e, pz=pz, rs=rs, ob=ob, gi=gi, t0=t0, tl=tl: e.scalar_tensor_tensor(
                        out=ob[:, t0:t0 + tl], in0=pz[:, 0:tl], scalar=vT[:, 48 + gi:49 + gi], in1=rs[:, 0:tl],
                        op0=ALU.mult, op1=ALU.mult), reads=[pz, vT, rs], writes=[ob])
                S.dma("sp", qkT_d[ci + pi, :, :], ob[:, :], reads=[ob], writes=[k_qkT])
            col += 512
            ci += 4
        vor = S.ring("vo", [128, 512], BF16, 3)
        o = 0
        while o < NTOK:
            w = min(512, NTOK - o)
            wt = load_w(col + o, w)
            for tt in range(T // 128):
                ps = C.psr.next()
                for kc in range(NCH):
                    mm(S, ps[:, 0:w], hT[kc][:, tt * 128:(tt + 1) * 128], wt[:, kc, 0:w], kc == 0, kc == NCH - 1,
                       [wt, hT[kc]], ps)
                vo = vor.next()
                S.op("act", lambda e, ps=ps, vo=vo, w=w: e.activation(out=vo[:, 0:w], in_=ps[:, 0:w], func=ACTF.Copy),
                     reads=[ps], writes=[vo])
                S.dma("sp", vtok_d[tt * 128:(tt + 1) * 128, o:o + w], vo[:, 0:w], reads=[vo], writes=[k_vtok])
            o += w
        C.finish()
    return nc


NKT = 18
import os as _os
ADBG = int(_os.environ.get("ADBG", "9"))
AQ = int(_os.environ.get("AQ", "9"))
NEG = -1e30


def tile_of_index(half, idx):
    if idx < 16:
        return idx if half == 0 else 15 - idx
    return 16 + (half if idx == 16 else 1 - half)


def b_key_tiles(g):
    def rs(r):
        return min(max(r - 4, 0), 24)
    lo = rs(2 * g) // 2
    hi = (rs(2 * g + 1) + 7) // 2
    return list(range(lo, hi + 1))


def b_slots(p):
    if p < 2:
        return [(u, p * 4 + u) for u in range(4)]
    return [(p - 2 + s_, 8 + s_) for s_ in range(5)]


def emit_attn8(C, S, psS, acc, t0, Qc, Kacc, Vacc, keys, esink, yT4, pTr, tmpr, osb_r, ytok_r, identb, misc_r):
    nk = len(keys)
    pTs = []
    if ADBG < 1:
        return
    for (kt, bias) in keys:
        pT = pTr.next()
        pTs.append(pT)
        pb = [psS.next(), psS.next()]
        for h in range(8):
            Kt, p0 = Kacc(h)
            assert p0 == (h % 2) * 64
            ps = pb[h % 2]
            S.op("pe", lambda e, ps=ps, Kt=Kt, p0=p0, kt=kt, h=h: e.matmul(
                ps[:, (h // 2) * 128:(h // 2 + 1) * 128], lhsT=Kt[p0:p0 + 64, kt * 128:(kt + 1) * 128],
                rhs=Qc[h // 2][p0:p0 + 64, t0:t0 + 128], start=True, stop=True),
                reads=[Kt, Qc[h // 2]], writes=[ps], inc=(h >= 6))
        for bank in range(2):
            ps = pb[bank]
            pT_v = pT[:, :].rearrange("p (r two q) -> p r two q", two=2, q=128)[:, :, bank, :]
            ps_v = ps[:, :].rearrange("p (r q) -> p r q", q=128)
            if bias is None:
                S.op("act", lambda e, ps_v=ps_v, pT_v=pT_v: e.activation(out=pT_v, in_=ps_v, func=ACTF.Exp, scale=0.125),
                     reads=[ps], writes=[pT])
            else:
                bt, bfn = bias
                tmp = tmpr.next()
                tmp_v = tmp[:, :].rearrange("p (r q) -> p r q", q=128)
                S.op("dve", lambda e, ps_v=ps_v, tmp_v=tmp_v, bfn=bfn, bank=bank: e.scalar_tensor_tensor(
                    out=tmp_v, in0=ps_v, scalar=0.125, in1=bfn(bank), op0=ALU.mult, op1=ALU.add),
                    reads=[ps, bt], writes=[tmp])
                S.op("act", lambda e, tmp_v=tmp_v, pT_v=pT_v: e.activation(out=pT_v, in_=tmp_v, func=ACTF.Exp),
                     reads=[tmp], writes=[pT])
    if ADBG < 2:
        return
    osb = osb_r.next()
    for bank in range(2):
        pso = acc[bank]
        for hh in range(4):
            h = bank * 4 + hh
            Vt, c0 = Vacc(h)
            for ki, (kt, _) in enumerate(keys):
                S.op("pe", lambda e, pso=pso, hh=hh, h=h, ki=ki, kt=kt, Vt=Vt, c0=c0: e.matmul(
                    pso[:, hh * 80:hh * 80 + 65], lhsT=pTs[ki][:, h * 128:(h + 1) * 128],
                    rhs=Vt[:, kt, c0:c0 + 65], start=(ki == 0), stop=(ki == nk - 1)),
                    reads=[pTs[ki], Vt], writes=[pso], inc=(hh == 3 and ki == nk - 1))
        S.op("act", lambda e, pso=pso, osb=osb, bank=bank: e.activation(
            out=osb[:, bank * 4:(bank + 1) * 4, :], in_=pso[:, 0:320].rearrange("p (h d) -> p h d", d=80)[:, :, 0:65], func=ACTF.Copy),
            reads=[pso], writes=[osb])
    if ADBG < 3:
        return
    den = misc_r.next()
    if esink is not None:
        S.op("dve", lambda e: e.tensor_tensor(out=den[:, 0:8], in0=osb[:, :, 64], in1=esink[:, 0:8], op=ALU.add),
             reads=[osb, esink], writes=[den])
        S.op("dve", lambda e: e.reciprocal(out=den[:, 0:8], in_=den[:, 0:8]), reads=[den], writes=[den])
    else:
        S.op("dve", lambda e: e.reciprocal(out=den[:, 0:8], in_=osb[:, :, 64]), reads=[osb], writes=[den])
    ytok = ytok_r.next()
    S.op("dve", lambda e: e.tensor_tensor(out=ytok[:, :, :], in0=osb[:, :, 0:64],
                                          in1=den[:, 0:8].unsqueeze(2).broadcast_to([128, 8, 64]), op=ALU.mult),
         reads=[osb, den], writes=[ytok])
    if ADBG < 4:
        return
    for c in range(4):
        ps = psS.next()
        S.op("pe", lambda e, ps=ps, c=c: e.matmul(ps[:, 0:128], lhsT=ytok[:, 2 * c:2 * c + 2, :].rearrange("p h d -> p (h d)"),
                                                  rhs=identb[:, :], start=True, stop=True),
             reads=[ytok, identb], writes=[ps])
        S.op("act", lambda e, ps=ps, c=c: e.activation(out=yT4[c][:, t0:t0 + 128], in_=ps[:, 0:128], func=ACTF.Copy),
             reads=[ps], writes=[yT4[c]])


def emit_mixers(C, D_in, yT, identb, which="ABCD"):
    S = C.S
    psS = Ring(C.psr.items[0:4])
    acc = C.psr.items[4:8]
    qT_d, KA_d, KB_d, KD_d, VA_d, VB_d, VD_d, CU_d = (D_in[k] for k in ("qT", "KA", "KB", "KD", "VA", "VB", "VD", "CU"))
    misc_r = S.ring("misc", [128, 16], F32, 4)

    if "A" in which:
        mk = S.mark()
        Qc = [S.sb("aq%d" % i, [128, T], BF16) for i in range(4)]
        for i in range(4):
            S.dma("sp", Qc[i][:, :], qT_d[i, :, :], writes=[Qc[i]])
        KA = [S.sb("KA%d" % i, [128, NKT * 128], BF16) for i in range(2)]
        for i in range(2):
            S.dma("sp", KA[i][:, :], KA_d[i, :, :], writes=[KA[i]])
        VA = S.sb("VA", [128, NKT, 160], BF16)
        S.dma("sp", VA[:, :, :], VA_d.rearrange("(k p) n -> p k n", p=128), writes=[VA])
        masks = S.sb("amask", [128, 2, 128], F32)
        S.dma("sp", masks[:, :, :], D_in["amask"].rearrange("m p n -> p m n"), writes=[masks])
        esink = S.sb("esink", [128, 8], F32)
        S.dma("sp", esink[:, :], D_in["a_sink"], writes=[esink])
        S.op("act", lambda e: e.activation(out=esink[:, :], in_=esink[:, :], func=ACTF.Exp), reads=[esink], writes=[esink])
        pTr = S.ring("pTa", [128, 1024], BF16, 10)
        tmpr = S.ring("tmpa", [128, 512], F32, 3)
        osb_r = S.ring("osba", [128, 8, 65], F32, 2)
        ytok_r = S.ring("ytoka", [128, 8, 64], BF16, 2)
        for qi in range(min(9, AQ)):
            if qi < 8:
                keys = []
                if qi > 0:
                    keys.append((qi - 1, (masks, lambda bank: masks[:, 0:1, :].broadcast_to([128, 4, 128]))))
                keys.append((qi, None))
                keys.append((qi + 1, (masks, lambda bank: masks[:, 1:2, :].broadcast_to([128, 4, 128]))))
                keys += [(16, None), (17, None)]
            else:
                keys = [(16, None), (17, None)]
            emit_attn8(C, S, psS, acc[0:2] if qi % 2 == 0 else acc[2:4], qi * 128, Qc, lambda h: (KA[h // 4], (h % 2) * 64),
                       lambda h: (VA, (h // 4) * 80), keys, esink, yT[0:4], pTr, tmpr, osb_r, ytok_r, identb, misc_r)
        S.barrier()
        S.flush()
        S.release(mk)

    if "B" in which:
        mk = S.mark()
        Qc = [S.sb("bq%d" % i, [128, T], BF16) for i in range(4)]
        for i in range(4):
            S.dma("sp", Qc[i][:, :], qT_d[4 + i, :, :], writes=[Qc[i]])
        KB = [S.sb("KB%d" % i, [128, NKT * 128], BF16) for i in range(4)]
        for i in range(4):
            S.dma("sp", KB[i][:, :], KB_d[i, :, :], writes=[KB[i]])
        VB = S.sb("VB", [128, NKT, 640], BF16)
        S.dma("sp", VB[:, :, :], VB_d.rearrange("(k p) n -> p k n", p=128), writes=[VB])
        nv = 13
        btab = S.sb("btab", [128, nv, 8, 128], F32)
        for v in range(nv):
            S.dma("sp", btab[:, v, :, :], D_in["btab"][v].rearrange("h p n -> p h n"), writes=[btab])
        pTr = S.ring("pTb", [128, 1024], BF16, 9)
        tmpr = S.ring("tmpb", [128, 512], F32, 3)
        osb_r = S.ring("osbb", [128, 8, 65], F32, 2)
        ytok_r = S.ring("ytokb", [128, 8, 64], BF16, 2)
        for qi in range(9):
            if qi < 8:
                keys = [(idx, (btab, lambda bank, v=v: btab[:, v, :, :].rearrange("p (r two) q -> p r two q", two=2)[:, :, bank, :]))
                        for (idx, v) in b_slots(qi)]
                keys += [(16, None), (17, None)]
            else:
                keys = [(16, None), (17, None)]
            emit_attn8(C, S, psS, acc[0:2] if qi % 2 == 0 else acc[2:4], qi * 128, Qc, lambda h: (KB[h // 2], (h % 2) * 64),
                       lambda h: (VB, h * 80), keys, None, yT[4:8], pTr, tmpr, osb_r, ytok_r, identb, misc_r)
        S.barrier()
        S.flush()
        S.release(mk)

    if "C" in which:
        mk = S.mark()
        CU = S.sb("CU", [128, NKT, 512], BF16)
        S.dma("sp", CU[:, :, :], CU_d.rearrange("(k p) n -> p k n", p=128), writes=[CU])
        Mt = S.sb("Mt", [128, 24, 128], BF16)
        S.dma("sp", Mt[:, :, :], D_in["Mtab"].rearrange("v p n -> p v n"), writes=[Mt])
        cw = S.sb("cw", [128, 4, 128], BF16)
        S.dma("pool", cw[:, :, :], D_in["c_w"].rearrange("g c e -> c g e"), writes=[cw])
        csT = load_vecT(C, [D_in["c_scale"]], "csT")
        poolT = S.ring("poolT", [128, T], BF16, 2)
        for grp in range(4):
            pl = poolT.next()
            for blk in range(3):
                tiles = list(range(blk * 4, min(blk * 4 + 4, 9)))
                ps = psS.next()
                for j, qi in enumerate(tiles):
                    if qi < 8:
                        terms = []
                        if qi > 0:
                            terms.append((qi - 1, 2))
                        terms.append((qi, 1 if qi == 0 else 0))
                        terms.append((qi + 1, 3))
                    else:
                        terms = [(16, 4), (17, 5)]
                    for ti, (kt, vv) in enumerate(terms):
                        S.op("pe", lambda e, ps=ps, j=j, kt=kt, vv=vv, ti=ti, n=len(terms), grp=grp: e.matmul(
                            ps[:, j * 128:(j + 1) * 128], lhsT=CU[:, kt, grp * 128:(grp + 1) * 128], rhs=Mt[:, grp * 6 + vv, :],
                            start=(ti == 0), stop=(ti == n - 1)), reads=[CU, Mt], writes=[ps],
                            inc=(ti == len(terms) - 1 and j == len(tiles) - 1))
                w = len(tiles) * 128
                S.op("act", lambda e, ps=ps, pl=pl, blk=blk, w=w: e.activation(out=pl[:, blk * 512:blk * 512 + w], in_=ps[:, 0:w],
                                                                              func=ACTF.Copy), reads=[ps], writes=[pl])
            for (tb0, tl) in TB:
                ps = psS.next()
                S.op("pe", lambda e, ps=ps, pl=pl, tb0=tb0, tl=tl, grp=grp: e.matmul(
                    ps[:, 0:tl], lhsT=cw[:, grp, :], rhs=pl[:, tb0:tb0 + tl], start=True, stop=True),
                    reads=[cw, pl], writes=[ps])
                S.op("act", lambda e, ps=ps, tb0=tb0, tl=tl, grp=grp: e.activation(
                    out=yT[8 + grp][:, tb0:tb0 + tl], in_=ps[:, 0:tl], func=ACTF.Copy, scale=csT[:, grp:grp + 1]),
                    reads=[ps, csT], writes=[yT[8 + grp]])
        S.barrier()
        S.flush()
        S.release(mk)

    if "D" in which:
        mk = S.mark()
        Qc = [S.sb("dq%d" % i, [128, T], BF16) for i in range(4)]
        for i in range(4):
            S.dma("sp", Qc[i][:, :], qT_d[8 + i, :, :], writes=[Qc[i]])
        KD = [S.sb("KD%d" % i, [128, NKT * 128], BF16) for i in range(4)]
        for i in range(4):
            S.dma("sp", KD[i][:, :], KD_d[i, :, :], writes=[KD[i]])
        VD = S.sb("VD", [128, NKT, 576], BF16)
        S.dma("sp", VD[:, :, :], VD_d.rearrange("(k p) n -> p k n", p=128), writes=[VD])
        dl = S.sb("dl", [128, 2, 2, 64], F32)
        S.dma("sp", dl[:, :, :, :], D_in["d_lambda"].rearrange("p (a b d) -> p a b d", b=2, d=64), writes=[dl])
        sub = S.sb("sub", [128, 128], F32)
        S.dma("sp", sub[:, :], D_in["d_subln"], writes=[sub])
        lamc = S.sb("lamc", [128, 2], F32)
        S.dma("sp", lamc[:, :], D_in["lamc"], writes=[lamc])
        S.op("dve", lambda e: e.tensor_scalar(out=sub[:, :], in0=sub[:, :], scalar1=lamc[:, 1:2], scalar2=None, op0=ALU.mult),
             reads=[sub, lamc], writes=[sub])
        prod = S.sb("dlp", [128, 2, 64], F32)
        S.op("dve", lambda e: e.tensor_tensor(out=prod[:, :, :], in0=dl[:, :, 0, :], in1=dl[:, :, 1, :], op=ALU.mult),
             reads=[dl], writes=[prod])
        lsum = S.sb("lsum", [128, 2], F32)
        S.op("dve", lambda e: e.tensor_reduce(out=lsum[:, :], in_=prod[:, :, :], axis=mybir.AxisListType.X, op=ALU.add),
             reads=[prod], writes=[lsum])
        S.op("act", lambda e: e.activation(out=lsum[:, :], in_=lsum[:, :], func=ACTF.Exp), reads=[lsum], writes=[lsum])
        nlam = S.sb("nlam", [128, 1], F32)
        S.op("dve", lambda e: e.tensor_tensor(out=nlam[:, :], in0=lsum[:, 1:2], in1=lsum[:, 0:1], op=ALU.subtract),
             reads=[lsum], writes=[nlam])
        S.op("dve", lambda e: e.tensor_tensor(out=nlam[:, :], in0=nlam[:, :], in1=lamc[:, 0:1], op=ALU.subtract),
             reads=[nlam, lamc], writes=[nlam])
        pTr = S.ring("pTd", [128, 512], BF16, 4)
        oacc_r = S.ring("oacc", [128, 4, 129], F32, 4)
        od_r = S.ring("od", [128, 4, 128], F32, 2)
        o2_r = S.ring("o2", [128, 4, 128], F32, 2)
        ytok_r = S.ring("ytokd", [128, 4, 128], BF16, 2)
        qblocks = [(0, 512, list(range(NKT))), (512, 512, list(range(NKT))), (1024, 128, [16, 17])]
        accsel = 0
        for (t0, n, kts) in qblocks:
            nq = n // 128
            for h in range(4):
                O = []
                for m in range(2):
                    ab = acc
                    def st_exp(ki, h=h, m=m, t0=t0, n=n):
                        kt = kts[ki]
                        ps = psS.next()
                        S.op("pe", lambda e: e.matmul(
                            ps[:, 0:n], lhsT=KD[h][m * 64:(m + 1) * 64, kt * 128:(kt + 1) * 128],
                            rhs=Qc[h][m * 64:(m + 1) * 64, t0:t0 + n], start=True, stop=True),
                            reads=[KD[h], Qc[h]], writes=[ps])
                        pT = pTr.next()
                        S.op("act", lambda e: e.activation(out=pT[:, 0:n], in_=ps[:, 0:n], func=ACTF.Exp, scale=0.125),
                             reads=[ps], writes=[pT])
                        return pT
                    pT_next = st_exp(0)
                    for ki, kt in enumerate(kts):
                        pT = pT_next
                        if ki + 1 < len(kts):
                            pT_next = st_exp(ki + 1)
                        for j in range(nq):
                            S.op("pe", lambda e, ab=ab, j=j, pT=pT, kt=kt, h=h, ki=ki, nk=len(kts): e.matmul(
                                ab[j][:, 0:129], lhsT=pT[:, j * 128:(j + 1) * 128],
                                rhs=VD[:, kt, h * 144:h * 144 + 129], start=(ki == 0), stop=(ki == nk - 1)),
                                reads=[pT, VD], writes=[ab[j]], inc=True)
                    oa = oacc_r.next()
                    for j in range(nq):
                        S.op("act", lambda e, ab=ab, oa=oa, j=j: e.activation(out=oa[:, j, :], in_=ab[j][:, 0:129], func=ACTF.Copy),
                             reads=[ab[j]], writes=[oa])
                    O.append(oa)
                r = misc_r.next()
                S.op("dve", lambda e, r=r, O=O, nq=nq: e.reciprocal(out=r[:, 0:nq], in_=O[0][:, 0:nq, 128]), reads=[O[0]], writes=[r])
                S.op("dve", lambda e, r=r, O=O, nq=nq: e.reciprocal(out=r[:, 4:4 + nq], in_=O[1][:, 0:nq, 128]), reads=[O[1], r], writes=[r])
                S.op("dve", lambda e, r=r, nq=nq: e.tensor_scalar(out=r[:, 4:4 + nq], in0=r[:, 4:4 + nq], scalar1=nlam[:, 0:1],
                                                                  scalar2=None, op0=ALU.mult), reads=[r, nlam], writes=[r])
                od = od_r.next()
                o2 = o2_r.next()
                S.op("dve", lambda e, od=od, O=O, r=r, nq=nq: e.tensor_tensor(
                    out=od[:, 0:nq, :], in0=O[0][:, 0:nq, 0:128], in1=r[:, 0:nq].unsqueeze(2).broadcast_to([128, nq, 128]), op=ALU.mult),
                    reads=[O[0], r], writes=[od])
                S.op("pool", lambda e, o2=o2, O=O, r=r, nq=nq: e.tensor_tensor(
                    out=o2[:, 0:nq, :], in0=O[1][:, 0:nq, 0:128], in1=r[:, 4:4 + nq].unsqueeze(2).broadcast_to([128, nq, 128]), op=ALU.mult),
                    reads=[O[1], r], writes=[o2])
                S.op("dve", lambda e, od=od, o2=o2, nq=nq: e.tensor_tensor(out=od[:, 0:nq, :], in0=od[:, 0:nq, :], in1=o2[:, 0:nq, :],
                                                                           op=ALU.add), reads=[od, o2], writes=[od])
                S.op("pool", lambda e, od=od, o2=o2, nq=nq: e.tensor_tensor(out=o2[:, 0:nq, :], in0=od[:, 0:nq, :], in1=od[:, 0:nq, :],
                                                                            op=ALU.mult), reads=[od, o2], writes=[o2])
                ss = misc_r.next()
                S.op("dve", lambda e, ss=ss, o2=o2, nq=nq: e.tensor_reduce(out=ss[:, 0:nq], in_=o2[:, 0:nq, :], axis=mybir.AxisListType.X,
                                                                           op=ALU.add), reads=[o2], writes=[ss])
                S.op("act", lambda e, ss=ss, nq=nq: e.activation(out=ss[:, 0:nq], in_=ss[:, 0:nq], func=ACTF.Sqrt, scale=1.0 / 128,
                                                                 bias=C.eps[:, 0:1]), reads=[ss, C.eps], writes=[ss])
                S.op("dve", lambda e, ss=ss, nq=nq: e.reciprocal(out=ss[:, 0:nq], in_=ss[:, 0:nq]), reads=[ss], writes=[ss])
                S.op("dve", lambda e, od=od, ss=ss, nq=nq: e.tensor_tensor(
                    out=od[:, 0:nq, :], in0=od[:, 0:nq, :], in1=ss[:, 0:nq].unsqueeze(2).broadcast_to([128, nq, 128]), op=ALU.mult),
                    reads=[od, ss], writes=[od])
                yt = ytok_r.next()
                S.op("dve", lambda e, od=od, yt=yt, nq=nq: e.tensor_tensor(
                    out=yt[:, 0:nq, :], in0=od[:, 0:nq, :], in1=sub[:, :].unsqueeze(1).broadcast_to([128, nq, 128]), op=ALU.mult),
                    reads=[od, sub], writes=[yt])
                for j in range(nq):
                    ps = psS.next()
                    S.op("pe", lambda e, ps=ps, yt=yt, j=j: e.matmul(ps[:, 0:128], lhsT=yt[:, j, :], rhs=identb[:, :], start=True, stop=True),
                         reads=[yt, identb], writes=[ps])
                    S.op("act", lambda e, ps=ps, h=h, j=j, t0=t0: e.activation(
                        out=yT[12 + h][:, t0 + j * 128:t0 + (j + 1) * 128], in_=ps[:, 0:128], func=ACTF.Copy),
                        reads=[ps], writes=[yT[12 + h]])
        S.barrier()
        S.flush()
        S.release(mk)


def emit_merge(C, hT, yT, mT, wg_d, wbr_d, bgT, boff):
    S = C.S
    wgr = S.ring("wg", [128, NCH, 512], BF16, 2)
    wbr_r = S.ring("wbr", [128, 4, 512], BF16, 2)
    gr = S.ring("gate", [128, 512], F32, 3)
    accr = S.ring("macc", [128, 512], F32, 2)
    tr = S.ring("mtmp", [128, 512], F32, 2)
    for dc in range(NCH):
        wg = wgr.next()
        S.dma("pool", wg[:, :, :], wg_d[dc], writes=[wg])
        wb = wbr_r.next()
        S.dma("pool", wb[:, :, :], wbr_d[dc], writes=[wb])
        for (t0, tl) in TB:
            macc = accr.next()
            for k in range(4):
                pg = C.psr.next()
                for kc in range(NCH):
                    mm(S, pg[:, 0:tl], wg[:, kc, k * 128:(k + 1) * 128], hT[kc][:, t0:t0 + tl], kc == 0, kc == NCH - 1,
                       [wg, hT[kc]], pg)
                g = gr.next()
                S.op("act", lambda e, pg=pg, g=g, k=k, dc=dc, tl=tl: e.activation(
                    out=g[:, 0:tl], in_=pg[:, 0:tl], func=ACTF.Sigmoid, bias=bgT[:, boff + k * 16 + dc:boff + k * 16 + dc + 1]),
                    reads=[pg, bgT], writes=[g])
                pp = C.psr.next()
                for kc in range(4):
                    mm(S, pp[:, 0:tl], wb[:, kc, k * 128:(k + 1) * 128], yT[4 * k + kc][:, t0:t0 + tl], kc == 0, kc == 3,
                       [wb, yT[4 * k + kc]], pp)
                if k == 0:
                    S.op("dve", lambda e, pp=pp, g=g, macc=macc, tl=tl: e.tensor_tensor(
                        out=macc[:, 0:tl], in0=pp[:, 0:tl], in1=g[:, 0:tl], op=ALU.mult), reads=[pp, g], writes=[macc])
                else:
                    tmp = tr.next()
                    S.op("dve", lambda e, pp=pp, g=g, tmp=tmp, tl=tl: e.tensor_tensor(
                        out=tmp[:, 0:tl], in0=pp[:, 0:tl], in1=g[:, 0:tl], op=ALU.mult), reads=[pp, g], writes=[tmp])
                    if k < 3:
                        S.op("pool", lambda e, macc=macc, tmp=tmp, tl=tl: e.tensor_tensor(
                            out=macc[:, 0:tl], in0=macc[:, 0:tl], in1=tmp[:, 0:tl], op=ALU.add), reads=[macc, tmp], writes=[macc])
                    else:
                        S.op("pool", lambda e, macc=macc, tmp=tmp, tl=tl, dc=dc, t0=t0: e.tensor_tensor(
                            out=mT[dc][:, t0:t0 + tl], in0=macc[:, 0:tl], in1=tmp[:, 0:tl], op=ALU.add),
                            reads=[macc, tmp], writes=[mT[dc]])


def emit_resid_gemm(C, xT, srcT, nk, w_tile_fn, gate, gsec, tblocks):
    S = C.S
    for dc in range(NCH):
        wt, lfn = w_tile_fn(dc)
        for (t0, tl) in tblocks:
            ps = C.psr.next()
            for kc in range(nk):
                mm(S, ps[:, 0:tl], lfn(kc), srcT[kc][:, t0:t0 + tl] if not callable(srcT) else srcT(kc, t0, tl)[0],
                   kc == 0, kc == nk - 1, [wt, srcT[kc] if not callable(srcT) else srcT(kc, t0, tl)[1]], ps)
            segs = []
            a, b = t0, t0 + tl
            if a < TL:
                segs.append((a, min(b, TL), 0))
            if b > TL:
                segs.append((max(a, TL), b, 1))
            for (sa, sb_, r) in segs:
                S.op("dve", lambda e, ps=ps, dc=dc, sa=sa, sb_=sb_, r=r, t0=t0: e.scalar_tensor_tensor(
                    out=xT[dc][:, sa:sb_], in0=ps[:, sa - t0:sb_ - t0], scalar=gate[:, gsec * 16 + dc, r:r + 1],
                    in1=xT[dc][:, sa:sb_], op0=ALU.mult, op1=ALU.add), reads=[ps, gate, xT[dc]], writes=[xT[dc]])


def build_rest(ffn_groups=3, stop=99, which="ABCD"):
    nc = bass.Bass("TRN2", target_bir_lowering=False)
    with contextlib.ExitStack() as st:
        C = Ctx(nc, st)
        S = C.S
        D_in = {}
        for (nm, shp, dt) in [("qT", [12, 128, T], BF16), ("KA", [2, 128, NKT * 128], BF16), ("KB", [4, 128, NKT * 128], BF16),
                              ("KD", [4, 128, NKT * 128], BF16), ("VA", [NKT * 128, 160], BF16), ("VB", [NKT * 128, 640], BF16),
                              ("VD", [NKT * 128, 576], BF16), ("CU", [NKT * 128, 512], BF16), ("amask", [2, 128, 128], F32),
                              ("a_sink", [128, 8], F32), ("btab", [13, 8, 128, 128], F32), ("Mtab", [24, 128, 128], BF16),
                              ("c_w", [4, 128, 128], F32), ("c_scale", [4, 128], F32), ("d_lambda", [128, 256], F32),
                              ("d_subln", [128, 128], F32), ("lamc", [128, 2], F32)]:
            D_in[nm] = C.din(nm, shp, dt)
        xT_d = C.din("xT", [D, T])
        hT_d = C.din("hT", [NCH, 128, T], BF16)
        cvec = C.din("cvec", [2, D])
        wada = C.din("wada", [D, 6 * D])
        vecsA = C.din("vecsA", [80, 128])
        vecsB = C.din("vecsB", [64, 128])
        ident_d = C.din("ident", [128, 128])
        wg_d = C.din("wg", [NCH, 128, NCH, 512])
        wbr_d = C.din("wbr", [NCH, 128, 4, 512])
        wo_d = C.din("wo", [4, 128, NCH, 512])
        w1_d = C.din("w1", [16, 128, NCH, 512])
        w2_d = C.din("w2", [NCH, 128, 64, 128])
        xo_d, k_xo = C.dout("xTo", [D, T])
        if stop < 99:
            dbg_d, k_dbg = C.dout("dbg", [NCH, 128, T], BF16)

        S.dma("sp", C.ident[:, :], ident_d, writes=[C.ident])
        identb = S.sb("identb", [128, 128], BF16)
        S.op("dve", lambda e: e.tensor_copy(out=identb[:, :], in_=C.ident[:, :]), reads=[C.ident], writes=[identb])
        vA = load_vecT(C, [vecsA], "vA")
        vB = load_vecT(C, [vecsB], "vB")
        siluT = emit_silu_c(C, cvec)
        modT = S.sb("modT2", [128, 4 * NCH, 2], F32)
        m0 = S.mark()

        def dump(tiles):
            for i, t in enumerate(tiles):
                S.dma("sp", dbg_d[i, :, :], t[:, :], reads=[t], writes=[k_dbg])

        yT = [S.sb("yT%d" % i, [128, T], BF16) for i in range(16)]
        m1 = S.mark()
        if which != "ABCD":
            for t in yT:
                S.op("pool", lambda e, t=t: e.memset(t[:, :], 0.0), writes=[t])
        emit_mixers(C, D_in, yT, identb, which)
        S.release(m1)
        if stop == 1:
            dump(yT)
            C.finish()
            return nc
        wring = S.ring("wt", [128, NCH, 512], BF16, 2)
        emit_ada(C, siluT, wada, vA, 0, 2, 4, wring, "modT2", modT=modT)
        hT = [S.sb("hT%d" % i, [128, T], BF16) for i in range(NCH)]
        for i in range(NCH):
            S.dma("sp", hT[i][:, :], hT_d[i, :, :], writes=[hT[i]])
        mT = [S.sb("mT%d" % i, [128, T], BF16, top=True) for i in range(NCH)]
        emit_merge(C, hT, yT, mT, wg_d, wbr_d, vB, 0)
        S.barrier()
        S.flush()
        if stop == 2:
            dump(mT)
            C.finish()
            return nc
        S.release(m0, bottom_only=True)
        xT = [S.sb("xT%d" % i, [128, T], F32) for i in range(NCH)]
        for dc in range(NCH):
            S.dma("sp", xT[dc][:, :], xT_d[dc * 128:(dc + 1) * 128, :], writes=[xT[dc]])
        wring = S.ring("wt2", [128, NCH, 512], BF16, 2)
        wstate = {}

        def wo_tile(dc):
            if dc % 4 == 0:
                wt = wring.next()
                S.dma("pool", wt[:, :, :], wo_d[dc // 4], writes=[wt])
                wstate["wo"] = wt
            wt = wstate["wo"]
            return wt, (lambda kc, wt=wt, dc=dc: wt[:, kc, (dc % 4) * 128:(dc % 4 + 1) * 128])
        emit_resid_gemm(C, xT, mT, NCH, wo_tile, modT, 0, TB)
        S.barrier()
        S.flush()
        S.release(S.mark(), top_only=False)
        S.hi = S.SB_HI
        if stop == 3:
            for dc in range(NCH):
                S.dma("sp", xo_d[dc * 128:(dc + 1) * 128, :], xT[dc][:, :], reads=[xT[dc]], writes=[k_xo])
            C.finish()
            return nc
        A2 = S.sb("A2", [128, NCH, 2], F32)
        B2 = S.sb("B2", [128, NCH, 2], F32)
        for r in range(2):
            S.op("dve", lambda e, r=r: e.scalar_tensor_tensor(out=A2[:, :, r], in0=modT[:, 32:48, r], scalar=1.0, in1=vA[:, 64:80],
                                                              op0=ALU.add, op1=ALU.mult), reads=[modT, vA], writes=[A2])
            S.op("dve", lambda e, r=r: e.tensor_copy(out=B2[:, :, r], in_=modT[:, 16:32, r]), reads=[modT], writes=[B2])
        sqr = S.ring("nsq", [128, T], BF16, 2)
        pss = [C.psr.next() for _ in TB]
        for dc in range(NCH):
            sq = sqr.next()
            S.op("act", lambda e, dc=dc, sq=sq: e.activation(out=sq[:, :], in_=xT[dc][:, :], func=ACTF.Square), reads=[xT[dc]], writes=[sq])
            for ti, (t0, tl) in enumerate(TB):
                mm(S, pss[ti][:, 0:tl], C.ones[:, :], sq[:, t0:t0 + tl], dc == 0, dc == NCH - 1, [C.ones, sq], pss[ti], inc=True)
        rs = S.sb("rs2", [128, T], F32)
        for ti, (t0, tl) in enumerate(TB):
            S.op("act", lambda e, ti=ti, t0=t0, tl=tl: e.activation(out=rs[:, t0:t0 + tl], in_=pss[ti][:, 0:tl], func=ACTF.Sqrt,
                                                                    scale=1.0 / D, bias=C.eps[:, 0:1]), reads=[pss[ti], C.eps], writes=[rs])
        S.op("dve", lambda e: e.reciprocal(out=rs[:, :], in_=rs[:, :]), reads=[rs], writes=[rs])
        S.barrier()
        S.flush()
        m3 = S.mark()
        gsz = T // ffn_groups
        assert gsz * ffn_groups == T and gsz % 128 == 0
        for gi in range(ffn_groups):
            S.release(m3)
            g0 = gi * gsz
            blocks = [(g0 + o, min(512, gsz - o)) for o in range(0, gsz, 512)]
            h2 = [S.sb("h2_%d" % i, [128, gsz], BF16) for i in range(NCH)]
            aT = [S.sb("aT%d" % i, [128, gsz], BF16) for i in range(64)]
            ntmp = S.ring("n2tmp", [128, gsz], F32, 2)
            for dc in range(NCH):
                tmp = ntmp.next()
                S.op("dve", lambda e, dc=dc, tmp=tmp: e.tensor_tensor(out=tmp[:, :], in0=xT[dc][:, g0:g0 + gsz], in1=rs[:, g0:g0 + gsz],
                                                                      op=ALU.mult), reads=[xT[dc], rs], writes=[tmp])
                a, b = g0, g0 + gsz
                segs = []
                if a < TL:
                    segs.append((a, min(b, TL), 0))
                if b > TL:
                    segs.append((max(a, TL), b, 1))
                for (sa, sb_, r) in segs:
                    S.op("act", lambda e, dc=dc, tmp=tmp, sa=sa, sb_=sb_, r=r: e.activation(
                        out=h2[dc][:, sa - g0:sb_ - g0], in_=tmp[:, sa - g0:sb_ - g0], func=ACTF.Identity,
                        scale=A2[:, dc, r:r + 1], bias=B2[:, dc, r:r + 1]), reads=[tmp, A2, B2], writes=[h2[dc]])
            relu_r = S.ring("relu", [128, 512], F32, 3)
            for blk in range(16):
                wt = wring.next()
                S.dma("pool", wt[:, :, :], w1_d[blk], writes=[wt])
                for j in range(4):
                    fc = blk * 4 + j
                    for (t0, tl) in blocks:
                        ps = C.psr.next()
                        for kc in range(NCH):
                            mm(S, ps[:, 0:tl], wt[:, kc, j * 128:(j + 1) * 128], h2[kc][:, t0 - g0:t0 - g0 + tl], kc == 0,
                               kc == NCH - 1, [wt, h2[kc]], ps)
                        rl = relu_r.next()
                        S.op("act", lambda e, ps=ps, rl=rl, tl=tl: e.activation(out=rl[:, 0:tl], in_=ps[:, 0:tl], func=ACTF.Relu),
                             reads=[ps], writes=[rl])
                        S.op("pool", lambda e, rl=rl, fc=fc, t0=t0, tl=tl: e.tensor_tensor(
                            out=aT[fc][:, t0 - g0:t0 - g0 + tl], in0=rl[:, 0:tl], in1=rl[:, 0:tl], op=ALU.mult),
                            reads=[rl], writes=[aT[fc]])

            def w2_tile(dc):
                wt = wring.next()
                S.dma("pool", wt[:, :, :].rearrange("p a b -> p (a b)"), w2_d[dc].rearrange("p a b -> p (a b)"), writes=[wt])
                return wt, (lambda fc, wt=wt: wt[:, fc // 4, (fc % 4) * 128:(fc % 4 + 1) * 128])
            emit_resid_gemm(C, xT, (lambda fc, t0, tl: (aT[fc][:, t0 - g0:t0 - g0 + tl], aT[fc])), 64, w2_tile, modT, 3, blocks)
            S.barrier()
            S.flush()
        for dc in range(NCH):
            S.dma("sp", xo_d[dc * 128:(dc + 1) * 128, :], xT[dc][:, :], reads=[xT[dc]], writes=[k_xo])
        C.finish()
    return nc


def rope_tables(half):
    t = own_token_rows(half)
    row = (t // 64).astype(np.float32)
    colp = (t % 64).astype(np.float32)
    inv = (np.float32(10000.0) ** (-np.arange(16, dtype=np.float32) / np.float32(16))).astype(np.float32)
    ar = row[None, :] * inv[:, None]
    ac = colp[None, :] * inv[:, None]
    c64 = np.concatenate([np.cos(ar), np.cos(ar), np.cos(ac), np.cos(ac)], 0)
    s64 = np.concatenate([-np.sin(ar), np.sin(ar), -np.sin(ac), np.sin(ac)], 0)
    Cc = np.ones((128, T), np.float32)
    Ss = np.zeros((128, T), np.float32)
    Cc[:, :TL] = np.concatenate([c64, c64], 0)
    Ss[:, :TL] = np.concatenate([s64, s64], 0)
    return Cc.astype(np.float32), Ss.astype(np.float32)


def bones_const():
    b = np.zeros((128, 128), np.float32)
    b[:64, :64] = 1
    b[64:, 64:] = 1
    return b


def proj_inputs(l, core, xT_core, inp, wp_l):
    b, half = core // 2, core % 2
    g = np.asarray(inp["qk_gain"][l], np.float32).reshape(6, 64)
    gains = np.concatenate([g, g], 1)
    gains_sw = gains[:, _swap_idx(128)]
    vecs = np.concatenate([inp["g_norm1"][l].reshape(16, 128), inp["b_ada"][l][0:D].reshape(16, 128),
                           inp["b_ada"][l][D:2 * D].reshape(16, 128), gains, gains_sw], 0).astype(np.float32)
    rc, rs = rope_tables(half)
    return {"xT": xT_core, "cvec": np.stack([inp["c"][b], inp["c_ctx"]]).astype(np.float32), "wada": inp["w_ada"][l],
            "vecs": vecs, "ident": np.eye(128, dtype=np.float32), "ropeC": rc, "ropeS": rs, "bones": bones_const(),
            "wp": wp_l}


def core_xT(x, ctx, core):
    b, half = core // 2, core % 2
    rows = np.concatenate([x[b][own_token_rows(half)], ctx[b, half * TC:(half + 1) * TC]], 0)
    return np.ascontiguousarray(rows.T)


BF = ml_dtypes.bfloat16


def amask_const(half):
    jj = np.arange(128)[:, None]
    ii = np.arange(128)[None, :]
    prev = np.where(ii <= jj, 0.0, NEG)
    nxt = np.where(jj <= ii, 0.0, NEG)
    return np.stack([prev, nxt] if half == 0 else [nxt, prev]).astype(np.float32)


def pool_tables(half):
    n = 512
    out = []
    t = np.arange(n)
    for w in (2, 4, 8, 16):
        lo = np.clip(t - w // 2, 0, n - 1)
        hi = np.clip(t + w - 1 - w // 2, 0, n - 1)
        M = ((t[:, None] >= lo[None, :]) & (t[:, None] <= hi[None, :])).astype(np.float64) / (hi - lo + 1)[None, :]
        M -= np.eye(n)
        interior, first, last = M[128:256, 128:256], M[0:128, 0:128], M[n - 128:n, n - 128:n]
        prev, nxt = M[0:128, 128:256], M[256:384, 128:256]
        if half == 0:
            out += [interior, first, prev, nxt, first, nxt]
        else:
            out += [interior, last, nxt, prev, last, prev]
    return np.stack(out).astype(np.float32).astype(BF)


def b_tables(rpb, half):
    pairs = {}
    for p in range(8):
        for (idx, v) in b_slots(p):
            pairs.setdefault(v, (tile_of_index(half, p), tile_of_index(half, idx)))
    tabs = np.empty((13, 8, 128, 128), np.float32)
    for v in range(13):
        g, u = pairs[v]
        q = g * 128 + np.arange(128)
        k = u * 128 + np.arange(128)
        r, c = q // 64, q % 64
        kr, kc = k // 64, k % 64
        rs = np.clip(r - 4, 0, 24)
        cs = np.clip(c - 8, 0, 48)
        valid = ((kr[:, None] >= rs[None, :]) & (kr[:, None] < rs[None, :] + 8)
                 & (kc[:, None] >= cs[None, :]) & (kc[:, None] < cs[None, :] + 16))
        ri = np.clip(kr[:, None] - r[None, :] + 7, 0, 14)
        ci = np.clip(kc[:, None] - c[None, :] + 15, 0, 30)
        for h in range(8):
            tabs[v, h] = np.where(valid, rpb[h][ri, ci], np.float32(NEG))
    return tabs


def rest_weights(inp, l):
    w_in = inp["w_in"][l]
    g = w_in[:, GATE0:].reshape(NCH, 128, 4, NCH, 128)
    wg = np.ascontiguousarray(g.transpose(3, 1, 0, 2, 4)).reshape(NCH, 128, NCH, 512)
    wb = inp["w_branch"][l].reshape(4, 4, 128, NCH, 128)
    wbr = np.ascontiguousarray(wb.transpose(3, 2, 1, 0, 4)).reshape(NCH, 128, 4, 512)
    wo = np.ascontiguousarray(inp["w_out"][l].reshape(NCH, 128, 4, 512).transpose(2, 1, 0, 3))
    w1 = np.ascontiguousarray(inp["w_ff1"][l].reshape(NCH, 128, 16, 512).transpose(2, 1, 0, 3))
    w2 = np.ascontiguousarray(inp["w_ff2"][l].reshape(64, 128, NCH, 128).transpose(2, 1, 0, 3))
    return dict(wg=wg, wbr=wbr, wo=wo, w1=w1, w2=w2)


def own_token_rows(half):
    return np.concatenate([np.arange(tile_of_index(half, p) * 128, tile_of_index(half, p) * 128 + 128) for p in range(8)])


def assemble_kv(pp, half):
    def tile_cm(j, tile):
        if tile < 16:
            h = tile // 8
            p = (tile % 8) if h == 0 else 7 - (tile % 8)
            return pp[h]["qkT"][j][:, p * 128:(p + 1) * 128]
        return pp[tile - 16]["qkT"][j][:, TL:]

    def tile_tm(tile):
        if tile < 16:
            h = tile // 8
            p = (tile % 8) if h == 0 else 7 - (tile % 8)
            return pp[h]["vtok"][p * 128:(p + 1) * 128]
        return pp[tile - 16]["vtok"][TL:]
    order = [tile_of_index(half, i) for i in range(NKT)]

    def full_cm(j):
        return np.concatenate([tile_cm(j, t) for t in order], 1)
    ak = full_cm(4)
    KA = np.stack([np.concatenate([ak[0:64], ak[0:64]], 0), np.concatenate([ak[64:128], ak[64:128]], 0)])
    KD = np.stack([full_cm(9 + i) for i in range(4)])
    KB = np.stack([full_cm(17 + i) for i in range(4)])
    v = np.concatenate([tile_tm(t) for t in order], 0)
    one = np.ones((v.shape[0], 1), BF)
    pad = np.zeros((v.shape[0], 15), BF)
    VA = np.concatenate([v[:, 0:64], one, pad, v[:, 64:128], one, pad], 1)
    VB = np.concatenate(sum([[v[:, 128 + 64 * h:128 + 64 * (h + 1)], one, pad] for h in range(8)], []), 1)
    CU = np.ascontiguousarray(v[:, 640:1152])
    VD = np.concatenate(sum([[v[:, 1152 + 128 * h:1152 + 128 * (h + 1)], one, pad] for h in range(4)], []), 1)
    return dict(KA=KA, KB=KB, KD=KD, VA=VA, VB=VB, VD=VD, CU=CU)


def rest_inputs(l, core, inp, xT_core, pown, kv, W):
    b, half = core // 2, core % 2
    lam_init = 0.8 - 0.6 * np.exp(-0.3 * l)
    qT = np.stack([pown["qkT"][j] for j in (0, 1, 2, 3, 13, 14, 15, 16, 5, 6, 7, 8)])
    ba = inp["b_ada"][l]
    vecsA = np.concatenate([ba[2 * D:6 * D].reshape(64, 128), inp["g_norm2"][l].reshape(16, 128)], 0).astype(np.float32)
    vecsB = inp["b_gate"][l].reshape(64, 128).astype(np.float32)
    d = dict(kv)
    d.update(W)
    d.update(qT=qT, amask=amask_const(half), a_sink=np.tile(inp["a_sink"][l][None, :], (128, 1)).astype(np.float32),
             btab=b_tables(inp["b_rpb"][l], half), Mtab=pool_tables(half), c_w=inp["c_w"][l], c_scale=inp["c_scale"][l].reshape(4, 128),
             d_lambda=np.tile(inp["d_lambda"][l].reshape(1, 256), (128, 1)).astype(np.float32),
             d_subln=np.tile(inp["d_subln"][l][None, :], (128, 1)).astype(np.float32),
             lamc=np.tile(np.array([[lam_init, 1 - lam_init]], np.float32), (128, 1)),
             xT=xT_core, hT=pown["hT"], cvec=np.stack([inp["c"][b], inp["c_ctx"]]).astype(np.float32), wada=inp["w_ada"][l],
             vecsA=vecsA, vecsB=vecsB, ident=np.eye(128, dtype=np.float32))
    return d


_PROGS = {}


def _prog(name, fn):
    if name not in _PROGS:
        _PROGS[name] = fn()
    return _PROGS[name]


def kernel(**inp):
    inp = {k: np.asarray(v) for k, v in inp.items()}
    x = inp["x"].astype(np.float32)
    ctx = inp["ctx"].astype(np.float32)
    ncores = 8
    xT = [core_xT(x, ctx, c) for c in range(ncores)]
    for l in range(2):
        wp = np.ascontiguousarray(inp["w_in"][l][:, proj_weight_cols()])
        nc_p = _prog("proj", build_proj)
        res = run_bass_kernel_spmd(nc_p, [proj_inputs(l, c, xT[c], inp, wp) for c in range(ncores)], core_ids=list(range(ncores)))
        pouts = [{k: np.asarray(v) for k, v in r.items()} for r in res.results]
        del wp
        W = rest_weights(inp, l)
        nc_r = _prog("rest", lambda: build_rest(ffn_groups=3))
        ins = []
        for c in range(ncores):
            b, half = c // 2, c % 2
            kv = assemble_kv([pouts[2 * b], pouts[2 * b + 1]], half)
            ins.append(rest_inputs(l, c, inp, xT[c], pouts[c], kv, W))
        res = run_bass_kernel_spmd(nc_r, ins, core_ids=list(range(ncores)))
        xT = [np.asarray(r["xTo"]) for r in res.results]
        del ins, W
    out = np.empty((4, SEQ, D), np.float32)
    for c in range(ncores):
        b, half = c // 2, c % 2
        out[b, own_token_rows(half)] = xT[c][:, :TL].T
    return out
```
